# Optimizing a Trainium2 kernel written in Bass

```python
import math
import jax, jax.numpy as jnp
from jax import lax
import numpy as np

D_MODEL = 2048
BATCH = 4
SEQ = 4096
DEPTH = 1
DEC_BATCH = 2
DEC_SEQ = 8192
PAST_LEN = 128

PLE_DIM = 256
GRID_W = 64
DN_HEADS = 8
DN_HEAD_DIM = 128
DN_WIDTH = DN_HEADS * DN_HEAD_DIM
CONV_K = 5
CHUNK = 64
NA_HEADS = 8
NA_HEAD_DIM = 128
NA_WIDTH = NA_HEADS * NA_HEAD_DIM
NA_KH_MAX = 8
NA_KW = 16
NA_COL_BLOCK = NA_KW
NA_COL_BAND = 2 * NA_KW
D_FF = ((8 * D_MODEL + 3 * 256 - 1) // (3 * 256)) * 256
EPS = 1e-6
N_IN = 4 * DN_WIDTH + 4 * DN_HEADS + 3 * NA_WIDTH + 2 * D_MODEL

kernel_name = "hybrid_deltanet_natten_encoder"


def rms_norm(x, g):
    xf = x.astype(jnp.float32)
    y = xf * lax.rsqrt(jnp.mean(xf * xf, axis=-1, keepdims=True) + EPS)
    return (y * g.astype(jnp.float32)).astype(x.dtype)


def l2_norm(x):
    xf = x.astype(jnp.float32)
    return xf * lax.rsqrt(jnp.sum(xf * xf, axis=-1, keepdims=True) + EPS)


def centred_depthwise_conv(x, w):
    c = x.shape[-1]
    pad = (CONV_K - 1) // 2
    return lax.conv_general_dilated(
        x, w[:, None, :].astype(x.dtype), window_strides=(1,), padding=[(pad, pad)],
        dimension_numbers=("NWC", "WIO", "NWC"), feature_group_count=c)


def gated_delta_rule_chunked(q, k, v, g, beta):
    b, t, h, dk = q.shape
    dv = v.shape[-1]
    n = t // CHUNK

    def chunks(a):
        a = a.reshape((b, n, CHUNK, h) + a.shape[3:])
        return jnp.moveaxis(a, 3, 1)

    q = chunks(q * dk ** -0.5)
    k = chunks(k)
    v = chunks(v)
    g = jnp.cumsum(chunks(g), axis=-1)
    beta = chunks(beta)
    k_beta = k * beta[..., None]

    idx = jnp.arange(CHUNK)
    lower_incl = idx[:, None] >= idx[None, :]
    lower_strict = idx[:, None] > idx[None, :]
    decay = jnp.exp(jnp.where(lower_incl, g[..., :, None] - g[..., None, :], -jnp.inf))
    a_mat = jnp.where(lower_strict, jnp.einsum("bhncd,bhnsd->bhncs", k_beta, k) * decay, 0.0)
    eye = jnp.eye(CHUNK, dtype=jnp.float32)
    t_mat = lax.linalg.triangular_solve(eye + a_mat, jnp.broadcast_to(eye, a_mat.shape),
                                        left_side=True, lower=True, unit_diagonal=True)
    value = jnp.einsum("bhncs,bhnsd->bhncd", t_mat, v * beta[..., None])
    k_cumdecay = jnp.einsum("bhncs,bhnsd->bhncd", t_mat, k_beta * jnp.exp(g)[..., None])
    attn_intra = jnp.einsum("bhncd,bhnsd->bhncs", q, k) * decay
    q_dec = q * jnp.exp(g)[..., None]
    g_last = g[..., -1]
    k_dec = k * jnp.exp(g_last[..., None] - g)[..., None]

    def step(s, inp):
        value_c, kcd_c, attn_c, qd_c, kd_c, gl_c = inp
        v_new = value_c - jnp.einsum("bhcd,bhde->bhce", kcd_c, s)
        o = jnp.einsum("bhcd,bhde->bhce", qd_c, s) + jnp.einsum("bhcs,bhse->bhce", attn_c, v_new)
        s = s * jnp.exp(gl_c)[..., None, None] + jnp.einsum("bhcd,bhce->bhde", kd_c, v_new)
        return s, o

    xs = tuple(jnp.moveaxis(a, 2, 0) for a in (value, k_cumdecay, attn_intra, q_dec, k_dec, g_last))
    s0 = jnp.zeros((b, h, dk, dv), jnp.float32)
    _, o = lax.scan(step, s0, xs)
    o = jnp.moveaxis(o, 0, 2).reshape(b, h, t, dv)
    return jnp.moveaxis(o, 1, 2)


def deltanet_branch(qkv, z, b_fb, a_fb, conv_w, a_log, dt_bias, norm_g):
    bsz, t, _ = qkv.shape
    qkv = jax.nn.silu(centred_depthwise_conv(qkv, conv_w))
    q, k, v = jnp.split(qkv, 3, axis=-1)
    q = l2_norm(q.reshape(bsz, t, DN_HEADS, DN_HEAD_DIM))
    k = l2_norm(k.reshape(bsz, t, DN_HEADS, DN_HEAD_DIM))
    v = v.reshape(bsz, t, DN_HEADS, DN_HEAD_DIM).astype(jnp.float32)
    beta = jax.nn.sigmoid(b_fb.astype(jnp.float32)).reshape(bsz, t, 2, DN_HEADS)
    g = -jnp.exp(a_log.astype(jnp.float32)) * jax.nn.softplus(
        a_fb.astype(jnp.float32).reshape(bsz, t, 2, DN_HEADS) + dt_bias.astype(jnp.float32))
    o_fwd = gated_delta_rule_chunked(q, k, v, g[:, :, 0], beta[:, :, 0])
    flip = lambda a: jnp.flip(a, axis=1)
    o_bwd = flip(gated_delta_rule_chunked(flip(q), flip(k), flip(v), flip(g[:, :, 1]), flip(beta[:, :, 1])))
    o = rms_norm(o_fwd + o_bwd, norm_g) * jax.nn.silu(
        z.astype(jnp.float32).reshape(bsz, t, DN_HEADS, DN_HEAD_DIM))
    return o.reshape(bsz, t, DN_WIDTH).astype(qkv.dtype)


def neighbourhood_attention(q, k, v, q_norm_g, k_norm_g, rpb):
    bsz, t, _ = q.shape
    rows = t // GRID_W
    kh = min(NA_KH_MAX, rows)
    h, dh = NA_HEADS, NA_HEAD_DIM
    grid = lambda a: a.reshape(bsz, rows, GRID_W, h, dh)
    qg = grid(rms_norm(q.reshape(bsz, t, h, dh), q_norm_g) * dh ** -0.5)
    kg = grid(rms_norm(k.reshape(bsz, t, h, dh), k_norm_g))
    vg = grid(v)

    n_cb = GRID_W // NA_COL_BLOCK
    qcol = np.arange(GRID_W).reshape(n_cb, NA_COL_BLOCK)
    band_start = np.clip(np.arange(n_cb) * NA_COL_BLOCK - NA_KW // 2, 0, GRID_W - NA_COL_BAND)
    kcol = band_start[:, None] + np.arange(NA_COL_BAND)
    win_start = np.clip(qcol - NA_KW // 2, 0, GRID_W - NA_KW)
    col_mask = (kcol[:, None, :] >= win_start[..., None]) & (kcol[:, None, :] < win_start[..., None] + NA_KW)
    dx_idx = np.clip(kcol[:, None, :] - qcol[..., None], -(NA_KW - 1), NA_KW - 1) + NA_KW - 1
    rpb_cols = rpb[:, :, dx_idx]
    k_cols = kg[:, :, kcol]
    v_cols = vg[:, :, kcol]
    mask = col_mask[:, :, None, :]

    def row_step(r):
        r0 = jnp.clip(r - kh // 2, 0, rows - kh)
        kb = lax.dynamic_slice_in_dim(k_cols, r0, kh, axis=1)
        vb = lax.dynamic_slice_in_dim(v_cols, r0, kh, axis=1)
        qr = lax.dynamic_index_in_dim(qg, r, axis=1, keepdims=False).reshape(bsz, n_cb, NA_COL_BLOCK, h, dh)
        s = jnp.einsum("bnqhd,bknchd->bhnqkc", qr, kb).astype(jnp.float32)
        dy_idx = r0 + jnp.arange(kh) - r + NA_KH_MAX - 1
        bias = jnp.take(rpb_cols, dy_idx, axis=1).transpose(0, 2, 3, 1, 4)
        s = jnp.where(mask, s + bias.astype(jnp.float32), -jnp.inf)
        p = jax.nn.softmax(s.reshape(s.shape[:4] + (kh * NA_COL_BAND,)), axis=-1)
        p = p.reshape(s.shape).astype(vb.dtype)
        o = jnp.einsum("bhnqkc,bknchd->bnqhd", p, vb)
        return o.reshape(bsz, GRID_W, h * dh)

    out = lax.map(row_step, jnp.arange(rows))
    return jnp.moveaxis(out, 0, 1).reshape(bsz, t, NA_WIDTH)


def encoder_layer(x, p, norm1_g, w_in, dn_conv_w, dn_a_log, dn_dt_bias, dn_norm_g,
                  na_q_norm_g, na_k_norm_g, na_rpb, w_branch_a, w_branch_b, w_out,
                  norm2_g, w_ff_gate, w_ff_up, w_ff_down, norm3_g, w_ple_gate, w_ple_proj):
    h = rms_norm(x, norm1_g)
    proj = h @ w_in
    sizes = [3 * DN_WIDTH, DN_WIDTH, 2 * DN_HEADS, 2 * DN_HEADS,
             NA_WIDTH, NA_WIDTH, NA_WIDTH, D_MODEL, D_MODEL]
    offsets = np.cumsum(sizes)[:-1].tolist()
    dn_qkv, dn_z, dn_b, dn_a, na_q, na_k, na_v, gate_a, gate_b = jnp.split(proj, offsets, axis=-1)
    y_a = deltanet_branch(dn_qkv, dn_z, dn_b, dn_a, dn_conv_w, dn_a_log, dn_dt_bias, dn_norm_g)
    y_b = neighbourhood_attention(na_q, na_k, na_v, na_q_norm_g, na_k_norm_g, na_rpb)
    merged = jax.nn.sigmoid(gate_a) * (y_a @ w_branch_a) + jax.nn.sigmoid(gate_b) * (y_b @ w_branch_b)
    x = x + merged @ w_out
    h2 = rms_norm(x, norm2_g)
    x = x + (jax.nn.silu(h2 @ w_ff_gate) * (h2 @ w_ff_up)) @ w_ff_down
    x = x + jax.nn.sigmoid(rms_norm(x, norm3_g) @ w_ple_gate) * (p @ w_ple_proj)
    return x


def setup_inputs(seed: int = 0) -> dict:
    key = jax.random.key(seed)
    ks = jax.random.split(key, 24)
    f32 = jnp.float32
    nrm = lambda k, shape, scale: jax.random.normal(k, shape, f32) * scale
    gain = lambda k, shape: 1.0 + 0.1 * jax.random.normal(k, shape, f32)
    dt = jnp.exp(jax.random.uniform(ks[7], (DEPTH, 2, DN_HEADS), f32, math.log(1e-3), math.log(1e-1)))
    return {
        "x_prompt": nrm(ks[0], (BATCH, SEQ, D_MODEL), 1.0),
        "x_sample": nrm(ks[1], (DEC_BATCH, DEC_SEQ, D_MODEL), 1.0),
        "p_prompt": nrm(ks[2], (DEPTH, BATCH, SEQ, PLE_DIM), 1.0),
        "p_sample": nrm(ks[3], (DEPTH, DEC_BATCH, DEC_SEQ, PLE_DIM), 1.0),
        "norm1_g": gain(ks[4], (DEPTH, D_MODEL)),
        "w_in": nrm(ks[5], (DEPTH, D_MODEL, N_IN), D_MODEL ** -0.5),
        "dn_conv_w": nrm(ks[6], (DEPTH, CONV_K, 3 * DN_WIDTH), CONV_K ** -0.5),
        "dn_a_log": jnp.log(jax.random.uniform(ks[8], (DEPTH, 2, DN_HEADS), f32, 1.0, 16.0)),
        "dn_dt_bias": jnp.log(jnp.expm1(dt)),
        "dn_norm_g": gain(ks[9], (DEPTH, DN_HEAD_DIM)),
        "na_q_norm_g": gain(ks[10], (DEPTH, NA_HEAD_DIM)),
        "na_k_norm_g": gain(ks[11], (DEPTH, NA_HEAD_DIM)),
        "na_rpb": nrm(ks[12], (DEPTH, NA_HEADS, 2 * NA_KH_MAX - 1, 2 * NA_KW - 1), 0.1),
        "w_branch_a": nrm(ks[13], (DEPTH, DN_WIDTH, D_MODEL), DN_WIDTH ** -0.5),
        "w_branch_b": nrm(ks[14], (DEPTH, NA_WIDTH, D_MODEL), NA_WIDTH ** -0.5),
        "w_out": nrm(ks[15], (DEPTH, D_MODEL, D_MODEL), D_MODEL ** -0.5),
        "norm2_g": gain(ks[16], (DEPTH, D_MODEL)),
        "w_ff_gate": nrm(ks[17], (DEPTH, D_MODEL, D_FF), D_MODEL ** -0.5),
        "w_ff_up": nrm(ks[18], (DEPTH, D_MODEL, D_FF), D_MODEL ** -0.5),
        "w_ff_down": nrm(ks[19], (DEPTH, D_FF, D_MODEL), D_FF ** -0.5),
        "norm3_g": gain(ks[20], (DEPTH, D_MODEL)),
        "w_ple_gate": nrm(ks[21], (DEPTH, D_MODEL, D_MODEL), D_MODEL ** -0.5),
        "w_ple_proj": nrm(ks[22], (DEPTH, PLE_DIM, D_MODEL), PLE_DIM ** -0.5),
    }


def reference(x_prompt, x_sample, p_prompt, p_sample, norm1_g, w_in, dn_conv_w, dn_a_log, dn_dt_bias,
              dn_norm_g, na_q_norm_g, na_k_norm_g, na_rpb, w_branch_a, w_branch_b, w_out, norm2_g,
              w_ff_gate, w_ff_up, w_ff_down, norm3_g, w_ple_gate, w_ple_proj):
    def trunk(x, p):
        for i in range(DEPTH):
            x = encoder_layer(x, p[i], norm1_g[i], w_in[i], dn_conv_w[i], dn_a_log[i], dn_dt_bias[i],
                              dn_norm_g[i], na_q_norm_g[i], na_k_norm_g[i], na_rpb[i], w_branch_a[i],
                              w_branch_b[i], w_out[i], norm2_g[i], w_ff_gate[i], w_ff_up[i], w_ff_down[i],
                              norm3_g[i], w_ple_gate[i], w_ple_proj[i])
        return x

    y_prompt = trunk(x_prompt, p_prompt)
    y_sample = trunk(x_sample, p_sample)
    return (y_prompt, y_sample)
```

```python
import math
import os
from contextlib import ExitStack
import numpy as np
import concourse.bass as bass
import concourse.mybir as mybir
from concourse.bass_utils import run_bass_kernel_spmd

F32 = mybir.dt.float32
BF16 = mybir.dt.bfloat16
ALU = mybir.AluOpType
AF = mybir.ActivationFunctionType
AX = mybir.AxisListType

D = 2048
KC = 16
TOK = 4096
NIN = 11296
DFF = 5632
EPS = 1e-6
NEG = -30000.0
ENGS = ("pe", "act", "dve", "pool", "sp")


class Reg:
    __slots__ = ("w", "r", "x")

    def __init__(self, x=False):
        self.w = None
        self.r = []
        self.x = x


def regs(n):
    return [Reg() for _ in range(n)]


class Op:
    __slots__ = ("eng", "fn", "deps", "dma", "sig", "sem", "val")

    def __init__(self, eng, fn, dma):
        self.eng = eng
        self.fn = fn
        self.deps = []
        self.dma = dma
        self.sig = False
        self.sem = None
        self.val = 0


class Rec:
    NRING = 12
    NSEM = 0

    def __init__(self, nc):
        self.nc = nc
        self.ops = {e: [] for e in ENGS}

    def op(self, eng, fn, reads=(), writes=(), dma=False):
        o = Op(eng, fn, dma)
        deps = {}
        for r in reads:
            if r.w is not None:
                deps[id(r.w)] = r.w
            if r.x and r.r and r.r[-1].eng != eng:
                deps[id(r.r[-1])] = r.r[-1]
        for w in writes:
            if w.w is not None:
                deps[id(w.w)] = w.w
            for rr in w.r:
                deps[id(rr)] = rr
        for d in deps.values():
            if d is o:
                continue
            if (not d.dma) and d.eng == eng and eng == "pe":
                continue
            o.deps.append(d)
        for r in reads:
            r.r.append(o)
        for w in writes:
            w.w = o
            w.r = []
        self.ops[eng].append(o)
        return o

    def dma(self, q, out, in_, rd=(), wr=()):
        return self.op(q, lambda e: e.dma_start(out=out, in_=in_), rd, wr, dma=True)

    def mm(self, out, lhsT, rhs, start=True, stop=True, rd=(), wr=()):
        return self.op("pe", lambda e: e.matmul(out, lhsT=lhsT, rhs=rhs, start=start, stop=stop), rd, wr)

    def tr(self, out, in_, ident, rd=(), wr=()):
        return self.op("pe", lambda e: e.transpose(out, in_, ident), rd, wr)

    def act(self, out, in_, func, rd=(), wr=(), **kw):
        return self.op("act", lambda e: e.activation(out=out, in_=in_, func=func, **kw), rd, wr)

    def tt(self, eng, out, in0, in1, op, rd=(), wr=()):
        return self.op(eng, lambda e: e.tensor_tensor(out=out, in0=in0, in1=in1, op=op), rd, wr)

    def ts(self, eng, out, in0, s1, s2, op0, op1=None, rd=(), wr=()):
        if op1 is None:
            return self.op(eng, lambda e: e.tensor_scalar(out=out, in0=in0, scalar1=s1, scalar2=None, op0=op0), rd, wr)
        return self.op(eng, lambda e: e.tensor_scalar(out=out, in0=in0, scalar1=s1, scalar2=s2, op0=op0, op1=op1), rd, wr)

    def stt(self, eng, out, in0, scalar, in1, op0, op1, rd=(), wr=()):
        return self.op(eng, lambda e: e.scalar_tensor_tensor(out=out, in0=in0, scalar=scalar, in1=in1, op0=op0, op1=op1), rd, wr)

    def copy(self, eng, out, in_, rd=(), wr=()):
        if eng == "act":
            return self.op("act", lambda e: e.copy(out=out, in_=in_), rd, wr)
        return self.op(eng, lambda e: e.tensor_copy(out=out, in_=in_), rd, wr)

    def recip(self, out, in_, rd=(), wr=()):
        return self.op("dve", lambda e: e.reciprocal(out=out, in_=in_), rd, wr)

    def memset(self, eng, ap, val, wr=()):
        return self.op(eng, lambda e: e.memset(ap, val), (), wr)

    def emit(self, st):
        nc = self.nc
        for e in ENGS:
            for o in self.ops[e]:
                for d in o.deps:
                    d.sig = True
        Rec.NSEM += 1
        pfx = "p%d_" % Rec.NSEM
        esem = {e: st.enter_context(nc.semaphore(pfx + "e_" + e)) for e in ENGS}
        rings = {e: [st.enter_context(nc.semaphore(pfx + "d_%s%d" % (e, i))) for i in range(self.NRING)] for e in ("sp", "pool", "act")}
        ecnt = {e: 0 for e in ENGS}
        rcnt = {e: [0] * self.NRING for e in rings}
        rnext = {e: 0 for e in rings}
        prev = {}
        for e in ENGS:
            for o in self.ops[e]:
                if o.dma:
                    j = rnext[e]
                    rnext[e] = (j + 1) % self.NRING
                    rcnt[e][j] += 16
                    o.sem = rings[e][j]
                    o.val = rcnt[e][j]
                    o.sig = True
                    prev[id(o)] = (o.sem, o.val - 16)
                elif o.sig:
                    ecnt[e] += 1
                    o.sem = esem[e]
                    o.val = ecnt[e]
        final = {e: [(rings[e][j], rcnt[e][j]) for j in range(self.NRING) if rcnt[e][j] > 0] for e in rings}
        block = st.enter_context(nc.Block())
        hw = {"pe": block.tensor, "act": block.scalar, "dve": block.vector, "pool": block.gpsimd, "sp": block.sync}
        for e in ENGS:
            def body(eng, ops=self.ops[e], fin=final.get(e, [])):
                waited = {}
                for o in ops:
                    need = {}
                    for d in o.deps:
                        k = id(d.sem)
                        if need.get(k, (None, 0))[1] < d.val:
                            need[k] = (d.sem, d.val)
                    if o.dma:
                        s, v = prev[id(o)]
                        if v > 0 and need.get(id(s), (None, 0))[1] < v:
                            need[id(s)] = (s, v)
                    for k, (s, v) in need.items():
                        if waited.get(k, 0) < v:
                            eng.wait_ge(s, v)
                            waited[k] = v
                    ins = o.fn(eng)
                    if o.sig:
                        ins.then_inc(o.sem, 16 if o.dma else 1)
                for s, v in fin:
                    if waited.get(id(s), 0) < v:
                        eng.wait_ge(s, v)
            hw[e](body)


class Ring:
    def __init__(self, items):
        self.items = items
        self.i = 0

    def next(self):
        it = self.items[self.i % len(self.items)]
        self.i += 1
        return it


class Prog:
    def __init__(self, debug=None, lite=False, mode=None):
        self.debug = debug
        self.lite = lite
        self.mode = mode
        nc = self.nc = bass.Bass("TRN2", target_bir_lowering=False)

        def di(n, s, dt=F32):
            if lite and n.startswith("w_") or (lite and n in ("xo", "xp", "pT")):
                s = (128, 128)
            return nc.dram_tensor(n, list(s), dt, kind="ExternalInput").ap()
        ds = lambda n, s, dt: nc.dram_tensor(n, list(s), dt, kind="Internal").ap()
        self.xo = di("xo", (D, TOK))
        self.xp = di("xp", (D, TOK))
        self.pT = di("pT", (256, TOK))
        self.w_in = di("w_in", (D, NIN))
        self.w_ba = di("w_ba", (D, 32))
        self.cst = di("cst", (128, 2048))
        self.coefc = di("coefc", (128, 2048))
        self.nab = di("nab", (128, 8 * 26 * 128))
        self.w_bra = di("w_bra", (1024, D))
        self.w_brb = di("w_brb", (1024, D))
        self.w_out = di("w_out", (D, D))
        self.w_fg = di("w_fg", (D, DFF))
        self.w_fu = di("w_fu", (D, DFF))
        self.w_fd = di("w_fd", (DFF, D))
        self.w_pg = di("w_pg", (D, D))
        self.w_pp = di("w_pp", (256, D))
        self.yT = nc.dram_tensor("yT", [D, TOK], F32, kind=("Internal" if mode == "A" else "ExternalOutput")).ap()
        dbg = debug is not None
        BIN = ("s_cq", "s_ck", "s_cv", "s_z", "s_yb", "s_ga", "s_gb", "s_coef")

        def dso(n, s, dt):
            if mode == "A":
                kind = "ExternalOutput" if n in BIN else "Internal"
            elif mode == "B":
                kind = "ExternalInput" if n in BIN else "Internal"
            else:
                kind = "ExternalOutput" if dbg else "Internal"
            return nc.dram_tensor(n, list(s), dt, kind=kind).ap()
        self.s_dqkv = dso("s_dqkv", (3072, 8192), BF16)
        self.s_z = dso("s_z", (1024, TOK), BF16)
        self.s_ga = dso("s_ga", (D, TOK), BF16)
        self.s_gb = dso("s_gb", (D, TOK), BF16)
        self.s_nq = dso("s_nq", (1024, TOK), BF16)
        self.s_nk = dso("s_nk", (1024, TOK + 256), BF16)
        self.s_nv = dso("s_nv", (TOK + 256, 1024), BF16)
        self.s_cq = dso("s_cq", (1024, TOK), BF16)
        self.s_ck = dso("s_ck", (1024, 8192), BF16)
        self.s_cv = dso("s_cv", (1024, 8192), BF16)
        self.s_oX = dso("s_oX", (TOK, 1024), F32)
        self.s_ya = dso("s_ya", (1024, TOK), BF16)
        self.s_yb = dso("s_yb", (1024, TOK), BF16)
        self.s_x1 = ds("s_x1", (D, TOK), F32)
        self.s_x2 = ds("s_x2", (D, TOK), F32)
        self.s_ba = dso("s_ba", (128, 64 * 32), F32)
        self.s_coef = dso("s_coef", (128, 8 * 1024), F32)
        self.r_scr = {}

    def sreg(self, name, i):
        k = (name, i)
        if k not in self.r_scr:
            self.r_scr[k] = Reg()
        return self.r_scr[k]

    def build(self, upto=99, only=None):
        nc = self.nc
        if only is not None:
            sel = lambda i: i in only
        else:
            sel = lambda i: upto >= i
        with ExitStack() as g:
            sb = lambda n, s, dt: g.enter_context(nc.sbuf_tensor(n, list(s), dt))
            self.cs = sb("cs", (128, 2048), F32)
            self.identB = sb("identB", (128, 128), BF16)
            self.onesB = sb("onesB", (128, 128), BF16)
            self.onesD = sb("onesD", (128, 128), BF16)
            self.onesH = sb("onesH", (128, 128), BF16)
            self.gqs = sb("gqs", (128, 1), F32)
            self.batok = sb("batok", (128, 64, 32), F32)
            self.phase_const()
            if sel(1):
                self.phase_p1()
            if sel(2):
                self.phase_na()
            with ExitStack() as g2:
                self.coef = g2.enter_context(nc.sbuf_tensor("coef", [128, 8, 1024], F32))
                if sel(3):
                    self.phase_dnprep()
                if sel(4):
                    self.phase_dnscan()
            if sel(5):
                self.phase_out()
        return nc

    def c_ident(self):
        return self.cs[:, 0:128]

    def c_mask(self, i):
        return self.cs[:, 128 + 128 * i: 256 + 128 * i]

    def c_tri(self, i):
        return self.cs[:, 640 + 128 * i: 768 + 128 * i]

    def c_ones(self):
        return self.cs[:, 1152:1280]

    def c_g(self, which, k):
        return self.cs[:, 1280 + 16 * which + k: 1281 + 16 * which + k]

    def c_col(self, i):
        return self.cs[:, 1328 + i: 1329 + i]

    def c_conv(self, ch, j):
        return self.cs[:, 1344 + ch * 5 + j: 1345 + ch * 5 + j]

    def phase_const(self):
        nc = self.nc
        with ExitStack() as st:
            R = Rec(nc)
            r = Reg()
            R.dma("sp", self.cs[:], self.cst, wr=[r])
            R.copy("dve", self.identB[:], self.c_ident(), rd=[r])
            R.memset("pool", self.onesB[:], 1.0)
            R.memset("pool", self.onesD[:], 1.0 / D)
            R.memset("pool", self.onesH[:], 1.0 / 128)
            R.ts("dve", self.gqs[:], self.c_col(1), 128.0 ** -0.5, None, ALU.mult, rd=[r])
            R.emit(st)
        nc.all_engine_barrier()

    def norm_tile(self, R, src_dram, t0, ntok, xn, r_xn, gsel, xsr, sqr, psr, rstd, r_rstd, sub=256):
        for s in range(ntok // sub):
            xs, r_xs = xsr.next()
            for q4 in range(4):
                R.dma("sp", xs[:, q4 * 4:(q4 + 1) * 4, :],
                      src_dram[q4 * 512:(q4 + 1) * 512, t0 + s * sub: t0 + (s + 1) * sub].rearrange("(k p) n -> p k n", p=128),
                      wr=[r_xs[q4]])
            ps, r_ps = psr.next()
            for k in range(KC):
                sq, r_sq = sqr.next()
                R.act(sq[:, 0:sub], xs[:, k, :], AF.Square, rd=[r_xs[k // 4]], wr=[r_sq])
                R.mm(ps[:, 0:sub], self.onesD[:], sq[:, 0:sub], start=(k == 0), stop=(k == KC - 1), rd=[r_sq], wr=[r_ps])
            R.act(rstd[:, 0:sub], ps[:, 0:sub], AF.Sqrt, rd=[r_ps], wr=[r_rstd], bias=EPS, scale=1.0)
            R.recip(rstd[:, 0:sub], rstd[:, 0:sub], rd=[r_rstd], wr=[r_rstd])
            for k in range(KC):
                R.stt("dve", xn[:, k, s * sub:(s + 1) * sub], xs[:, k, :], self.c_g(gsel, k), rstd[:, 0:sub], ALU.mult, ALU.mult,
                      rd=[r_xs[k // 4], r_rstd], wr=[r_xn[k]])

    def phase_p1(self):
        nc = self.nc
        TT = 2048
        with ExitStack() as st:
            R = Rec(nc)
            sb = lambda n, s, dt: st.enter_context(nc.sbuf_tensor(n, list(s), dt))
            pst = lambda n, s, dt: st.enter_context(nc.psum_tensor(n, list(s), dt))
            xn = sb("xn", (128, KC, TT), BF16)
            r_xn = regs(KC)
            xsr = Ring([(sb("xs%d" % i, (128, KC, 256), F32), regs(4)) for i in range(2)])
            sqr = Ring([(sb("sq%d" % i, (128, 512), BF16), Reg()) for i in range(3)])
            rstd = sb("rstd", (128, 512), F32)
            r_rstd = Reg()
            wstr = Ring([(sb("wst%d" % i, (128, KC, 256), F32), Reg()) for i in range(2)])
            wbfr = Ring([(sb("wbf%d" % i, (128, KC, 256), BF16), Reg()) for i in range(2)])
            stgr = Ring([(sb("stg%d" % i, (128, TT), BF16), Reg()) for i in range(2)])
            vstr = Ring([(sb("vst%d" % i, (128, 256), BF16), Reg()) for i in range(3)])
            hr = Ring([(sb("hr%d" % i, (128, 512), F32), Reg()) for i in range(2)])
            wba_s = sb("wba_s", (128, KC, 32), F32)
            wba_b = sb("wba_b", (128, KC, 32), BF16)
            banks = [pst("bk%d" % i, (128, 512), F32) for i in range(8)]
            psr = Ring([(banks[i], Reg()) for i in range(5)])
            psn = Ring([(banks[5], Reg()), (banks[6], Reg())])
            pss = Ring([(banks[7], Reg())])
            r_wba = Reg()
            R.dma("sp", wba_s[:], self.w_ba.rearrange("(k p) n -> p k n", p=128), wr=[r_wba])
            R.copy("dve", wba_b[:], wba_s[:], rd=[r_wba], wr=[r_wba])
            castn = [0]

            def load_w(c0, ncols=256):
                wst, r_wst = wstr.next()
                R.dma("sp", wst[:, :, 0:ncols], self.w_in[:, c0:c0 + ncols].rearrange("(k p) n -> p k n", p=128), wr=[r_wst])
                wbf, r_wbf = wbfr.next()
                eng = "pool" if castn[0] % 3 != 2 else "dve"
                castn[0] += 1
                R.copy(eng, wbf[:, :, 0:ncols], wst[:, :, 0:ncols], rd=[r_wst], wr=[r_wbf])
                return wbf, r_wbf

            evn = [0]

            def fm_job(c0, nchunks, kind, dst, row0, tcol0, n0, ntok, gcol=None):
                for b in range(nchunks // 2):
                    wbf, r_wbf = load_w(c0 + b * 256)
                    for ci in range(2):
                        stg, r_stg = stgr.next()
                        nsub = (ntok + 511) // 512
                        for s4 in range(nsub):
                            w_ = min(512, ntok - s4 * 512)
                            ps, r_ps = psr.next()
                            for k in range(KC):
                                R.mm(ps[:, 0:w_], wbf[:, k, ci * 128:(ci + 1) * 128], xn[:, k, n0 + s4 * 512: n0 + s4 * 512 + w_],
                                     start=(k == 0), stop=(k == KC - 1), rd=[r_wbf, r_xn[k]], wr=[r_ps])
                            o = stg[:, s4 * 512: s4 * 512 + w_]
                            if kind == "id":
                                evn[0] += 1
                                R.copy("act" if evn[0] % 2 else "dve", o, ps[:, 0:w_], rd=[r_ps], wr=[r_stg])
                            elif kind == "silu":
                                R.act(o, ps[:, 0:w_], AF.Silu, rd=[r_ps], wr=[r_stg])
                            elif kind == "sig":
                                R.act(o, ps[:, 0:w_], AF.Sigmoid, rd=[r_ps], wr=[r_stg])
                            else:
                                sq, r_sq = sqr.next()
                                R.act(sq[:, 0:w_], ps[:, 0:w_], AF.Square, rd=[r_ps], wr=[r_sq])
                                pn, r_pn = psn.next()
                                R.mm(pn[:, 0:w_], self.onesH[:], sq[:, 0:w_], rd=[r_sq], wr=[r_pn])
                                h_, r_h = hr.next()
                                R.act(h_[:, 0:w_], pn[:, 0:w_], AF.Sqrt, rd=[r_pn], wr=[r_h], bias=EPS, scale=1.0)
                                R.recip(h_[:, 0:w_], h_[:, 0:w_], rd=[r_h], wr=[r_h])
                                R.stt("dve", o, ps[:, 0:w_], gcol, h_[:, 0:w_], ALU.mult, ALU.mult, rd=[r_ps, r_h], wr=[r_stg])
                        ch = b * 2 + ci
                        R.dma("act", dst[row0 + ch * 128: row0 + (ch + 1) * 128, tcol0: tcol0 + ntok], stg[:, 0:ntok],
                              rd=[r_stg])

            def tm_v_job(n0, ntok, trow0):
                for b in range(4):
                    wbf, r_wbf = load_w(6176 + b * 256)
                    for tt in range(ntok // 128):
                        ps, r_ps = psr.next()
                        for k in range(KC):
                            R.mm(ps[:, 0:256], xn[:, k, n0 + tt * 128: n0 + (tt + 1) * 128], wbf[:, k, :],
                                 start=(k == 0), stop=(k == KC - 1), rd=[r_wbf, r_xn[k]], wr=[r_ps])
                        vs, r_vs = vstr.next()
                        evn[0] += 1
                        R.copy("act" if evn[0] % 2 else "dve", vs[:], ps[:, 0:256], rd=[r_ps], wr=[r_vs])
                        R.dma("act", self.s_nv[trow0 + tt * 128: trow0 + (tt + 1) * 128, b * 256:(b + 1) * 256], vs[:],
                              rd=[r_vs])

            def ba_job(ntok, tile0):
                for tt in range(ntok // 128):
                    ps, r_ps = pss.next()
                    for k in range(KC):
                        R.mm(ps[:, 0:32], xn[:, k, tt * 128:(tt + 1) * 128], wba_b[:, k, :], start=(k == 0), stop=(k == KC - 1),
                             rd=[r_wba, r_xn[k]], wr=[r_ps])
                    R.copy("dve", self.batok[:, tile0 + tt, :], ps[:, 0:32], rd=[r_ps], wr=[self.sreg("batok", 0)])

            for grp in range(2):
                src = self.xp if grp == 0 else self.xo
                for tt in range(2):
                    t0 = tt * TT
                    self.norm_tile(R, src, t0, TT, xn, r_xn, 0, xsr, sqr, psn, rstd, r_rstd)
                    ba_job(TT, grp * 32 + tt * 16)
                    if grp == 0:
                        fm_job(0, 24, "id", self.s_dqkv, 0, t0, 0, TT)
                        if tt == 1:
                            fm_job(5152, 8, "hn", self.s_nk, 0, 0, TT - 256, 256, gcol=self.c_col(2))
                            tm_v_job(TT - 256, 256, 0)
                    else:
                        fm_job(0, 24, "id", self.s_dqkv, 0, TOK + t0, 0, TT)
                        fm_job(3072, 8, "silu", self.s_z, 0, t0, 0, TT)
                        fm_job(4128, 8, "hn", self.s_nq, 0, t0, 0, TT, gcol=self.gqs[:, 0:1])
                        fm_job(5152, 8, "hn", self.s_nk, 0, 256 + t0, 0, TT, gcol=self.c_col(2))
                        tm_v_job(0, TT, 256 + t0)
                        fm_job(7200, 16, "sig", self.s_ga, 0, t0, 0, TT)
                        fm_job(9248, 16, "sig", self.s_gb, 0, t0, 0, TT)
            if self.debug is not None:
                R.dma("sp", self.s_ba, self.batok[:].rearrange("p a b -> p (a b)"), rd=[self.sreg("batok", 0)])
            R.emit(st)
        nc.all_engine_barrier()

    def phase_na(self):
        nc = self.nc
        with ExitStack() as st:
            R = Rec(nc)
            sb = lambda n, s, dt: st.enter_context(nc.sbuf_tensor(n, list(s), dt))
            pst = lambda n, s, dt: st.enter_context(nc.psum_tensor(n, list(s), dt))
            V = sb("naV", (128, 34, 1024), BF16)
            r_V = regs(34)
            for i in range(34):
                R.dma("sp", V[:, i, :], self.s_nv[i * 128:(i + 1) * 128, :], wr=[r_V[i]])
            bias = sb("nabias", (128, 8, 26, 128), BF16)
            r_bias = regs(8)
            bst = Ring([(sb("nabst%d" % i, (128, 26 * 128), F32), Reg()) for i in range(2)])
            for h in range(8):
                b_, r_b = bst.next()
                R.dma("sp", b_[:], self.nab[:, h * 3328:(h + 1) * 3328], wr=[r_b])
                R.copy("pool", bias[:, h, :, :].rearrange("p a b -> p (a b)"), b_[:], rd=[r_b], wr=[r_bias[h]])
            kr = Ring([(sb("nak%d" % i, (128, 8, 768), BF16), Reg()) for i in range(2)])
            qr = Ring([(sb("naq%d" % i, (128, 8, 128), BF16), Reg()) for i in range(2)])
            ptr = Ring([(sb("napt%d" % i, (128, 6, 128), BF16), Reg()) for i in range(3)])
            rsr = Ring([(sb("nars%d" % i, (128, 128), F32), Reg()) for i in range(2)])
            ybr = Ring([(sb("nayb%d" % i, (128, 8, 128), BF16), Reg()) for i in range(2)])
            banks = [pst("nbk%d" % i, (128, 512), F32) for i in range(8)]
            sr = Ring([((banks[0], banks[1]), (Reg(), Reg())), ((banks[2], banks[3]), (Reg(), Reg()))])
            outr = Ring([(banks[4], Reg()), (banks[5], Reg())])
            sumr = Ring([(banks[6], Reg()), (banks[7], Reg())])
            for p in range(32):
                if p == 0:
                    row0, nt, bi = -4, 6, 5
                elif p == 1:
                    row0, nt, bi = -2, 5, 11
                elif p == 30:
                    row0, nt, bi = 56, 4, 16
                elif p == 31:
                    row0, nt, bi = 56, 4, 20
                else:
                    row0, nt, bi = 2 * p - 4, 5, 0
                kc0 = (row0 + 4) * 64
                vi0 = (row0 + 4) // 2
                kt, r_kt = kr.next()
                R.dma("sp", kt[:, :, 0:nt * 128], self.s_nk[:, kc0:kc0 + nt * 128].rearrange("(h p) n -> p h n", p=128), wr=[r_kt])
                qt, r_qt = qr.next()
                R.dma("sp", qt[:], self.s_nq[:, p * 128:(p + 1) * 128].rearrange("(h p) n -> p h n", p=128), wr=[r_qt])
                yb, r_yb = ybr.next()
                for h in range(8):
                    (sA, sB), (r_sA, r_sB) = sr.next()
                    for t in range(nt):
                        bk, rb = (sA, r_sA) if t < 4 else (sB, r_sB)
                        o = bk[:, (t % 4) * 128:(t % 4 + 1) * 128]
                        R.mm(o, kt[:, h, t * 128:(t + 1) * 128], qt[:, h, :], start=True, stop=False, rd=[r_kt, r_qt], wr=[rb])
                        R.mm(o, self.identB[:], bias[:, h, bi + t, :], start=False, stop=True, rd=[r_bias[h]], wr=[rb])
                    pt, r_pt = ptr.next()
                    n1 = min(nt, 4)
                    R.act(pt[:, 0:n1, :].rearrange("p a b -> p (a b)"), sA[:, 0:n1 * 128], AF.Exp, rd=[r_sA], wr=[r_pt])
                    if nt > 4:
                        R.act(pt[:, 4:nt, :].rearrange("p a b -> p (a b)"), sB[:, 0:(nt - 4) * 128], AF.Exp, rd=[r_sB], wr=[r_pt])
                    ob, r_ob = outr.next()
                    sm, r_sm = sumr.next()
                    for t in range(nt):
                        R.mm(ob[:, 0:128], V[:, vi0 + t, h * 128:(h + 1) * 128], pt[:, t, :], start=(t == 0), stop=(t == nt - 1),
                             rd=[r_V[vi0 + t], r_pt], wr=[r_ob])
                    for t in range(nt):
                        R.mm(sm[:, 0:128], self.onesB[:], pt[:, t, :], start=(t == 0), stop=(t == nt - 1), rd=[r_pt], wr=[r_sm])
                    rs, r_rs = rsr.next()
                    R.recip(rs[:], sm[:, 0:128], rd=[r_sm], wr=[r_rs])
                    R.tt("dve", yb[:, h, :], ob[:, 0:128], rs[:], ALU.mult, rd=[r_ob, r_rs], wr=[r_yb])
                R.dma("act", self.s_yb[:, p * 128:(p + 1) * 128].rearrange("(h p) n -> p h n", p=128), yb[:], rd=[r_yb])
            R.emit(st)
        nc.all_engine_barrier()

    def phase_out(self):
        nc = self.nc
        TT = 512
        with ExitStack() as st:
            R = Rec(nc)
            sb = lambda n, s, dt: st.enter_context(nc.sbuf_tensor(n, list(s), dt))
            pst = lambda n, s, dt: st.enter_context(nc.psum_tensor(n, list(s), dt))
            big = sb("big", (128, 44, TT), BF16)
            r_big = regs(44)
            x1f = sb("x1f", (128, KC, TT), F32)
            r_x1f = regs(KC)
            hb = sb("hb", (128, KC, TT), BF16)
            r_hb = regs(KC)
            wstr = Ring([(sb("owst%d" % i, (128, 4096), F32), Reg()) for i in range(2)])
            wbfr = Ring([(sb("owbf%d" % i, (128, 4096), BF16), Reg()) for i in range(3)])
            gtr = Ring([(sb("ogt%d" % i, (128, TT), BF16), Reg()) for i in range(4)])
            xrr = Ring([(sb("oxr%d" % i, (128, TT), F32), Reg()) for i in range(2)])
            tmr = Ring([(sb("otm%d" % i, (128, TT), F32), Reg()) for i in range(4)])
            sqr = Ring([(sb("osq%d" % i, (128, TT), BF16), Reg()) for i in range(2)])
            ysr = Ring([(sb("oys%d" % i, (128, TT), F32), Reg()) for i in range(2)])
            rstd = sb("orstd", (128, TT), F32)
            r_rstd = Reg()
            pstg = sb("opst", (128, 2, TT), F32)
            pb = sb("opb", (128, 2, TT), BF16)
            r_pstg, r_pb = Reg(), Reg()
            banks = [pst("obk%d" % i, (128, 512), F32) for i in range(8)]
            psr = Ring([(banks[i], Reg()) for i in range(6)])
            pn = (banks[6], Reg())
            castn = [0]

            def wload(w, k0, nk, c0, ncols):
                wst, r_wst = wstr.next()
                v = wst[:, 0:nk * ncols].rearrange("p (k n) -> p k n", n=ncols)
                R.dma("sp", v, w[k0 * 128:(k0 + nk) * 128, c0:c0 + ncols].rearrange("(k p) n -> p k n", p=128), wr=[r_wst])
                wbf, r_wbf = wbfr.next()
                vb = wbf[:, 0:nk * ncols].rearrange("p (k n) -> p k n", n=ncols)
                eng = "pool" if castn[0] % 3 != 2 else "dve"
                castn[0] += 1
                R.copy(eng, wbf[:, 0:nk * ncols], wst[:, 0:nk * ncols], rd=[r_wst], wr=[r_wbf])
                return vb, r_wbf

            def stats_finish(gsel):
                R.act(rstd[:], pn[0][:], AF.Sqrt, rd=[pn[1]], wr=[r_rstd], bias=EPS, scale=1.0)
                R.recip(rstd[:], rstd[:], rd=[r_rstd], wr=[r_rstd])
                for m in range(KC):
                    R.stt("dve", hb[:, m, :], hb[:, m, :], self.c_g(gsel, m), rstd[:], ALU.mult, ALU.mult, rd=[r_hb[m], r_rstd], wr=[r_hb[m]])

            def stat_acc(src, r_src, m):
                sq, r_sq = sqr.next()
                R.act(sq[:], src, AF.Square, rd=[r_src], wr=[r_sq])
                R.mm(pn[0][:], self.onesD[:], sq[:], start=(m == 0), stop=(m == KC - 1), rd=[r_sq], wr=[pn[1]])

            for tt in range(TOK // TT):
                t0 = tt * TT
                for k in range(8):
                    R.dma("sp", big[:, k, :], self.s_ya[k * 128:(k + 1) * 128, t0:t0 + TT], wr=[r_big[k]])
                    R.dma("sp", big[:, 8 + k, :], self.s_yb[k * 128:(k + 1) * 128, t0:t0 + TT], wr=[r_big[8 + k]])
                for mb in range(8):
                    wa, r_wa = wload(self.w_bra, 0, 8, mb * 256, 256)
                    wb_, r_wb = wload(self.w_brb, 0, 8, mb * 256, 256)
                    for ci in range(2):
                        m = mb * 2 + ci
                        pa, r_pa = psr.next()
                        for k in range(8):
                            R.mm(pa[:], wa[:, k, ci * 128:(ci + 1) * 128], big[:, k, :], start=(k == 0), stop=(k == 7), rd=[r_wa, r_big[k]], wr=[r_pa])
                        pb_, r_pb_ = psr.next()
                        for k in range(8):
                            R.mm(pb_[:], wb_[:, k, ci * 128:(ci + 1) * 128], big[:, 8 + k, :], start=(k == 0), stop=(k == 7), rd=[r_wb, r_big[8 + k]], wr=[r_pb_])
                        ga, r_ga = gtr.next()
                        R.dma("sp", ga[:], self.s_ga[m * 128:(m + 1) * 128, t0:t0 + TT], wr=[r_ga])
                        gb, r_gb = gtr.next()
                        R.dma("sp", gb[:], self.s_gb[m * 128:(m + 1) * 128, t0:t0 + TT], wr=[r_gb])
                        t1, r_t1 = tmr.next()
                        R.tt("dve", t1[:], pa[:], ga[:], ALU.mult, rd=[r_pa, r_ga], wr=[r_t1])
                        t2, r_t2 = tmr.next()
                        R.tt("dve", t2[:], pb_[:], gb[:], ALU.mult, rd=[r_pb_, r_gb], wr=[r_t2])
                        R.tt("pool", big[:, 16 + m, :], t1[:], t2[:], ALU.add, rd=[r_t1, r_t2], wr=[r_big[16 + m]])
                for mb in range(8):
                    w, r_w = wload(self.w_out, 0, 16, mb * 256, 256)
                    for ci in range(2):
                        m = mb * 2 + ci
                        ps, r_ps = psr.next()
                        for k in range(KC):
                            R.mm(ps[:], w[:, k, ci * 128:(ci + 1) * 128], big[:, 16 + k, :], start=(k == 0), stop=(k == KC - 1), rd=[r_w, r_big[16 + k]], wr=[r_ps])
                        xr, r_xr = xrr.next()
                        R.dma("sp", xr[:], self.xo[m * 128:(m + 1) * 128, t0:t0 + TT], wr=[r_xr])
                        R.tt("dve", x1f[:, m, :], ps[:], xr[:], ALU.add, rd=[r_ps, r_xr], wr=[r_x1f[m]])
                        stat_acc(x1f[:, m, :], r_x1f[m], m)
                        R.copy("pool", hb[:, m, :], x1f[:, m, :], rd=[r_x1f[m]], wr=[r_hb[m]])
                stats_finish(1)
                for jb in range(22):
                    wg, r_wg = wload(self.w_fg, 0, 16, jb * 256, 256)
                    wu, r_wu = wload(self.w_fu, 0, 16, jb * 256, 256)
                    for ci in range(2):
                        j = jb * 2 + ci
                        pg, r_pg = psr.next()
                        for k in range(KC):
                            R.mm(pg[:], wg[:, k, ci * 128:(ci + 1) * 128], hb[:, k, :], start=(k == 0), stop=(k == KC - 1), rd=[r_wg, r_hb[k]], wr=[r_pg])
                        pu, r_pu = psr.next()
                        for k in range(KC):
                            R.mm(pu[:], wu[:, k, ci * 128:(ci + 1) * 128], hb[:, k, :], start=(k == 0), stop=(k == KC - 1), rd=[r_wu, r_hb[k]], wr=[r_pu])
                        t1, r_t1 = tmr.next()
                        R.act(t1[:], pg[:], AF.Silu, rd=[r_pg], wr=[r_t1])
                        R.tt("dve", big[:, j, :], t1[:], pu[:], ALU.mult, rd=[r_t1, r_pu], wr=[r_big[j]])
                for m in range(KC):
                    w0, r_w0 = wload(self.w_fd, 0, 22, m * 128, 128)
                    w1, r_w1 = wload(self.w_fd, 22, 22, m * 128, 128)
                    ps, r_ps = psr.next()
                    for j in range(44):
                        w_, r_w_ = (w0, r_w0) if j < 22 else (w1, r_w1)
                        R.mm(ps[:], w_[:, j % 22, :], big[:, j, :], start=(j == 0), stop=(j == 43), rd=[r_w_, r_big[j]], wr=[r_ps])
                    R.tt("dve", x1f[:, m, :], ps[:], x1f[:, m, :], ALU.add, rd=[r_ps, r_x1f[m]], wr=[r_x1f[m]])
                    stat_acc(x1f[:, m, :], r_x1f[m], m)
                    R.copy("pool", hb[:, m, :], x1f[:, m, :], rd=[r_x1f[m]], wr=[r_hb[m]])
                stats_finish(2)
                R.dma("sp", pstg[:], self.pT[:, t0:t0 + TT].rearrange("(k p) n -> p k n", p=128), wr=[r_pstg])
                R.copy("pool", pb[:], pstg[:], rd=[r_pstg], wr=[r_pb])
                for mb in range(8):
                    wg, r_wg = wload(self.w_pg, 0, 16, mb * 256, 256)
                    wp, r_wp = wload(self.w_pp, 0, 2, mb * 256, 256)
                    for ci in range(2):
                        m = mb * 2 + ci
                        p1, r_p1 = psr.next()
                        for k in range(KC):
                            R.mm(p1[:], wg[:, k, ci * 128:(ci + 1) * 128], hb[:, k, :], start=(k == 0), stop=(k == KC - 1), rd=[r_wg, r_hb[k]], wr=[r_p1])
                        p2, r_p2 = psr.next()
                        for k in range(2):
                            R.mm(p2[:], wp[:, k, ci * 128:(ci + 1) * 128], pb[:, k, :], start=(k == 0), stop=(k == 1), rd=[r_wp, r_pb], wr=[r_p2])
                        t1, r_t1 = tmr.next()
                        R.act(t1[:], p1[:], AF.Sigmoid, rd=[r_p1], wr=[r_t1])
                        t2, r_t2 = tmr.next()
                        R.tt("dve", t2[:], t1[:], p2[:], ALU.mult, rd=[r_t1, r_p2], wr=[r_t2])
                        ys, r_ys = ysr.next()
                        R.tt("pool", ys[:], t2[:], x1f[:, m, :], ALU.add, rd=[r_t2, r_x1f[m]], wr=[r_ys])
                        R.dma("act", self.yT[m * 128:(m + 1) * 128, t0:t0 + TT], ys[:], rd=[r_ys])
            R.emit(st)
        nc.all_engine_barrier()

    def cview(self, i):
        return self.coef[:, i, :].rearrange("p (t d) -> p t d", d=16)

    def phase_dnprep(self):
        nc = self.nc
        with ExitStack() as st:
            R = Rec(nc)
            sb = lambda n, s, dt: st.enter_context(nc.sbuf_tensor(n, list(s), dt))
            pst = lambda n, s, dt: st.enter_context(nc.psum_tensor(n, list(s), dt))
            cf = sb("cf", (128, 2048), F32)
            tmp = [sb("ctmp%d" % i, (128, 1024), F32) for i in range(6)]
            banks = [pst("pbk%d" % i, (128, 512), F32) for i in range(8)]
            rc = Reg()
            R.dma("sp", cf[:], self.coefc, wr=[rc])
            ba = self.batok
            v3 = lambda t: t[:].rearrange("p (t d) -> p t d", d=16)
            bsrc, asrc = ba[:, :, 0:16], ba[:, :, 16:32]
            beta, gc, negg, gelb, bege, eglg = (self.coef[:, i, :] for i in range(6))
            S = dict(rd=[rc], wr=[rc])
            R.act(self.cview(0), bsrc, AF.Sigmoid, **S)
            R.act(tmp[0][:], beta, AF.Ln, **S)
            R.tt("dve", v3(tmp[1]), asrc, cf[:, 1024:2048].rearrange("p (t d) -> p t d", d=16), ALU.add, **S)
            R.act(tmp[2][:], tmp[1][:], AF.Abs, **S)
            R.act(tmp[2][:], tmp[2][:], AF.Exp, scale=-1.0, **S)
            R.act(tmp[2][:], tmp[2][:], AF.Ln, bias=1.0, scale=1.0, **S)
            R.ts("dve", tmp[1][:], tmp[1][:], 0.0, None, ALU.max, **S)
            R.tt("dve", tmp[1][:], tmp[1][:], tmp[2][:], ALU.add, **S)
            R.act(tmp[3][:], cf[:, 0:1024], AF.Exp, **S)
            R.stt("dve", tmp[1][:], tmp[1][:], -1.0, tmp[3][:], ALU.mult, ALU.mult, **S)
            g = tmp[1]
            for i, dstt in ((0, tmp[4]), (1, tmp[5])):
                for hf in range(2):
                    R.mm(banks[i * 2 + hf][:], self.c_tri(i), g[:, hf * 512:(hf + 1) * 512], **S)
                    R.copy("dve", dstt[:, hf * 512:(hf + 1) * 512], banks[i * 2 + hf][:], **S)
            cX, cY, g3 = v3(tmp[4]), v3(tmp[5]), v3(g)
            R.copy("dve", self.cview(1)[:, :, 0:8], cX[:, :, 0:8], **S)
            R.copy("dve", self.cview(1)[:, :, 8:16], cY[:, :, 8:16], **S)
            R.tt("dve", v3(tmp[2])[:, :, 0:8], cY[:, :, 0:8], g3[:, :, 0:8], ALU.subtract, **S)
            R.tt("dve", v3(tmp[2])[:, :, 8:16], cX[:, :, 8:16], g3[:, :, 8:16], ALU.subtract, **S)
            R.act(eglg, tmp[2][:], AF.Exp, **S)
            R.ts("dve", negg, gc, -1.0, None, ALU.mult, **S)
            R.tt("dve", gelb, gc, tmp[0][:], ALU.add, **S)
            R.act(tmp[3][:], gc, AF.Exp, **S)
            R.tt("dve", bege, beta, tmp[3][:], ALU.mult, **S)
            for e in range(2):
                for hf in range(2):
                    R.mm(banks[4 + e * 2 + hf][:], self.c_tri(2 + e), g[:, hf * 512:(hf + 1) * 512], **S)
                    R.act(self.coef[:, 6 + e, hf * 512:(hf + 1) * 512], banks[4 + e * 2 + hf][:], AF.Exp, **S)
            if self.debug is not None or self.mode == "A":
                R.dma("sp", self.s_coef, self.coef[:].rearrange("p a b -> p (a b)"), rd=[rc])
            SEG = 2048
            rawr = Ring([(sb("raw%d" % i, (128, SEG + 4), BF16), Reg()) for i in range(2)])
            accr = Ring([(sb("acc%d" % i, (128, SEG), F32), Reg()) for i in range(2)])
            slr = Ring([(sb("sl%d" % i, (128, SEG), F32), Reg()) for i in range(2)])
            stgr = Ring([(sb("cstg%d" % i, (128, SEG), BF16), Reg()) for i in range(2)])
            sqr = Ring([(sb("csq%d" % i, (128, 512), BF16), Reg()) for i in range(2)])
            rr = Ring([(sb("crr%d" % i, (128, 512), F32), Reg()) for i in range(2)])
            psn = Ring([(banks[0], rc), (banks[1], rc)])
            n = 0
            for ch in range(24):
                kind = ch // 8
                dst = (self.s_cq, self.s_ck, self.s_cv)[kind]
                hrow = (ch % 8) * 128
                for s0 in range(0, 8192, SEG):
                    if kind == 0 and s0 < TOK:
                        continue
                    raw, r_raw = rawr.next()
                    lo, hi = max(s0 - 2, 0), min(s0 + SEG + 2, 8192)
                    if s0 == 0:
                        R.memset("pool", raw[:, 0:2], 0.0, wr=[r_raw])
                    if s0 + SEG == 8192:
                        R.memset("pool", raw[:, SEG + 2:SEG + 4], 0.0, wr=[r_raw])
                    R.dma("sp", raw[:, lo - (s0 - 2): hi - (s0 - 2)], self.s_dqkv[ch * 128:(ch + 1) * 128, lo:hi], wr=[r_raw])
                    acc, r_acc = accr.next()
                    eng = "dve"
                    n += 1
                    R.ts(eng, acc[:], raw[:, 0:SEG], self.c_conv(ch, 0), None, ALU.mult, rd=[r_raw], wr=[r_acc])
                    for j in range(1, 5):
                        R.stt("dve", acc[:], raw[:, j:j + SEG], self.c_conv(ch, j), acc[:], ALU.mult, ALU.add, rd=[r_raw, r_acc], wr=[r_acc])
                    stg, r_stg = stgr.next()
                    if kind == 2:
                        R.act(stg[:], acc[:], AF.Silu, rd=[r_acc], wr=[r_stg])
                    else:
                        sl, r_sl = slr.next()
                        R.act(sl[:], acc[:], AF.Silu, rd=[r_acc], wr=[r_sl])
                        for q4 in range(SEG // 512):
                            sq, r_sq = sqr.next()
                            R.act(sq[:], sl[:, q4 * 512:(q4 + 1) * 512], AF.Square, rd=[r_sl], wr=[r_sq])
                            ps, r_ps = psn.next()
                            R.mm(ps[:], self.onesB[:], sq[:], rd=[r_sq], wr=[r_ps])
                            r_, r_r = rr.next()
                            if kind == 0:
                                R.act(r_[:], ps[:], AF.Sqrt, rd=[r_ps], wr=[r_r], bias=128.0 * EPS, scale=128.0)
                            else:
                                R.act(r_[:], ps[:], AF.Sqrt, rd=[r_ps], wr=[r_r], bias=EPS, scale=1.0)
                            R.recip(r_[:], r_[:], rd=[r_r], wr=[r_r])
                            R.tt("pool", stg[:, q4 * 512:(q4 + 1) * 512], sl[:, q4 * 512:(q4 + 1) * 512], r_[:], ALU.mult, rd=[r_sl, r_r], wr=[r_stg])
                    tcol = s0 - TOK if kind == 0 else s0
                    R.dma("act", dst[hrow:hrow + 128, tcol:tcol + SEG], stg[:], rd=[r_stg])
            R.emit(st)
        nc.all_engine_barrier()

    def phase_dnscan(self):
        nc = self.nc
        with ExitStack() as st:
            R = Rec(nc)
            sb = lambda n, s, dt: st.enter_context(nc.sbuf_tensor(n, list(s), dt))
            pst = lambda n, s, dt: st.enter_context(nc.psum_tensor(n, list(s), dt))

            def ring(name, shape, dt, n=2):
                return Ring([(sb("%s%d" % (name, i), shape, dt), Reg()) for i in range(n)])

            if self.mode == "B":
                R.dma("sp", self.coef[:].rearrange("p a b -> p (a b)"), self.s_coef)
            Sf = sb("Sf", (128, 8, 128), F32)
            Sb = sb("Sb", (128, 8, 128), BF16)
            r_S = regs(8)
            r_Sb = regs(8)
            R.memset("dve", Sf[:].rearrange("p a b -> p (a b)"), 0.0, wr=r_S)
            R.memset("pool", Sb[:].rearrange("p a b -> p (a b)"), 0.0, wr=r_Sb)
            qdz = []
            for i in range(2):
                a = sb("qdza%d" % i, (128, 128), BF16)
                b = sb("qdzb%d" % i, (128, 128), BF16)
                ra, rb = Reg(), Reg()
                R.memset("pool", a[:], 0.0, wr=[ra])
                R.memset("pool", b[:], 0.0, wr=[rb])
                qdz.append(((a, ra), (b, rb)))
            qdzr = Ring(qdz)
            kgr, vgr, qgr, zgr = (ring(nm, (128, 8, 512), BF16) for nm in ("kg", "vg", "qg", "zg"))
            kbgr, kdr, vbr = (ring(nm, (128, 128), BF16) for nm in ("kbg", "kd", "vb"))
            dgr = ring("dg", (128, 256), F32)
            xr_ = ring("xx", (128, 128), F32, 3)
            er_ = ring("ee", (128, 128), F32, 3)
            ur = ring("uu", (128, 128), BF16, 3)
            mr = ring("mm", (128, 128), BF16, 3)
            wr_ = ring("ww", (128, 128), BF16, 3)
            atr = ring("at", (128, 128), BF16)
            valr = ring("val", (128, 128), F32)
            kcr = ring("kc", (128, 128), BF16)
            vnr = ring("vn", (128, 128), BF16)
            osr = ring("os", (128, 8, 128), F32)
            oxr = ring("ox", (128, 8, 128), F32)
            sqo = sb("sqo", (128, 8, 128), F32)
            ssr = ring("ss", (128, 8), F32)
            onr = ring("on", (128, 8, 128), BF16)
            yar = ring("ya", (128, 8, 128), BF16)
            r_sqo = Reg()
            tpb = pst("tpb", (128, 1024), BF16)
            r_tpb = Reg(True)
            bk = [pst("sbk%d" % i, (128, 512), F32) for i in range(6)]
            r_bk = [Reg(True) for _ in range(6)]
            tyb_ = pst("tyb", (128, 1024), BF16)
            tyb = tyb_[:, 0:128]
            r_tyb = Reg()
            identF, onesF = self.c_ident(), self.c_ones()
            C = lambda i, t, dh: self.coef[:, i, t * 16 + dh: t * 16 + dh + 1]
            evn = [0]

            def alt():
                evn[0] += 1
                return "act" if evn[0] % 2 else "dve"

            STOP = int(os.environ.get("DN_STOP", "99"))

            def unit(t, toff, h, d, with_out, kg, r_kg, vg, r_vg, qg, r_qg):
                dh = d * 8 + h
                if STOP <= 0:
                    return
                kT = kg[:, h, toff:toff + 128]
                vT = vg[:, h, toff:toff + 128]
                mU, mA, mQ = (self.c_mask(0), self.c_mask(1), self.c_mask(2)) if d == 0 else (self.c_mask(1), self.c_mask(0), self.c_mask(3))
                R.tr(tpb[:, 0:128], kT, self.identB[:], rd=[r_kg], wr=[r_tpb])
                R.tr(tpb[:, 128:256], vT, self.identB[:], rd=[r_vg], wr=[r_tpb])
                if os.environ.get("DN_VAR", "") == "t":
                    return
                kbg, r_kbg = kbgr.next()
                kd, r_kd = kdr.next()
                vb, r_vb = vbr.next()
                VAR = os.environ.get("DN_VAR", "")
                if VAR == "z":
                    R.ts("dve", kbg[:], tpb[:, 0:128], 2.0, None, ALU.mult, rd=[r_tpb], wr=[r_kbg])
                    return
                if VAR == "k":
                    R.ts("dve", kbg[:], tpb[:, 0:128], C(4, t, dh), None, ALU.mult, rd=[r_tpb], wr=[r_kbg])
                    return
                if VAR == "v":
                    R.ts("dve", kbg[:], tpb[:, 0:128], C(4, t, dh), None, ALU.mult, rd=[r_tpb], wr=[r_kbg])
                    R.ts("dve", vb[:], tpb[:, 128:256], C(0, t, dh), None, ALU.mult, rd=[r_tpb], wr=[r_vb])
                    return
                if VAR == "d":
                    R.ts("dve", kbg[:], tpb[:, 0:128], C(4, t, dh), None, ALU.mult, rd=[r_tpb], wr=[r_kbg])
                    R.ts("dve", kd[:], tpb[:, 0:128], C(5, t, dh), None, ALU.mult, rd=[r_tpb], wr=[r_kd])
                    return
                R.ts("dve", kbg[:], tpb[:, 0:128], C(4, t, dh), None, ALU.mult, rd=[r_tpb], wr=[r_kbg])
                R.ts("dve", kd[:], tpb[:, 0:128], C(5, t, dh), None, ALU.mult, rd=[r_tpb], wr=[r_kd])
                R.ts("dve", vb[:], tpb[:, 128:256], C(0, t, dh), None, ALU.mult, rd=[r_tpb], wr=[r_vb])
                if STOP <= 1:
                    return
                R.mm(bk[0][:, 0:128], kT, kT, rd=[r_kg], wr=[r_bk[0]])
                if with_out:
                    qT = qg[:, h, toff:toff + 128]
                    R.mm(bk[0][:, 128:256], kT, qT, rd=[r_kg, r_qg], wr=[r_bk[0]])
                if STOP <= 2:
                    return
                dg, r_dg = dgr.next()
                R.ts("pool", dg[:, 0:128], identF, C(1, t, dh), None, ALU.mult, wr=[r_dg])
                R.ts("pool", dg[:, 128:256], identF, C(3, t, dh), None, ALU.mult, wr=[r_dg])
                R.mm(bk[1][:, 0:256], onesF, dg[:], rd=[r_dg], wr=[r_bk[1]])
                Gb, Gb2 = bk[1][:, 0:128], bk[1][:, 128:256]
                if STOP <= 3:
                    return
                xx, r_xx = xr_.next()
                R.stt("dve", xx[:], Gb2, C(2, t, dh), mU, ALU.add, ALU.add, rd=[r_bk[1]], wr=[r_xx])
                ee, r_ee = er_.next()
                R.act(ee[:], xx[:], AF.Exp, rd=[r_xx], wr=[r_ee])
                U, r_U = ur.next()
                R.tt("dve", U[:], bk[0][:, 0:128], ee[:], ALU.mult, rd=[r_bk[0], r_ee], wr=[r_U])
                xx, r_xx = xr_.next()
                R.stt("dve", xx[:], Gb, -1.0, mA, ALU.mult, ALU.add, rd=[r_bk[1]], wr=[r_xx])
                ee, r_ee = er_.next()
                R.act(ee[:], xx[:], AF.Exp, bias=C(3, t, dh), scale=1.0, rd=[r_xx], wr=[r_ee])
                M, r_M = mr.next()
                R.tt("dve", M[:], bk[0][:, 0:128], ee[:], ALU.mult, rd=[r_bk[0], r_ee], wr=[r_M])
                if with_out:
                    xx, r_xx = xr_.next()
                    R.stt("dve", xx[:], Gb, C(2, t, dh), mQ, ALU.add, ALU.add, rd=[r_bk[1]], wr=[r_xx])
                    ee, r_ee = er_.next()
                    R.act(ee[:], xx[:], AF.Exp, rd=[r_xx], wr=[r_ee])
                    at, r_at = atr.next()
                    R.tt("dve", at[:], bk[0][:, 128:256], ee[:], ALU.mult, rd=[r_bk[0], r_ee], wr=[r_at])
                    eg, r_eg = er_.next()
                    R.act(eg[:], Gb, AF.Exp, rd=[r_bk[1]], wr=[r_eg])
                    (qa, r_qa), (qb, r_qb) = qdzr.next()
                    R.tt("dve", qa[:, 0:64], qT[:, 0:64], eg[:, 0:64], ALU.mult, rd=[r_qg, r_eg], wr=[r_qa])
                    R.tt("dve", qb[:, 64:128], qT[:, 64:128], eg[:, 64:128], ALU.mult, rd=[r_qg, r_eg], wr=[r_qb])
                if STOP <= 4:
                    return
                W, r_W = wr_.next()
                R.tt("pool", W[:], self.identB[:], U[:], ALU.subtract, rd=[r_U], wr=[r_W])
                for j in range(5):
                    R.mm(bk[2][:, 0:128], U[:], M[:], rd=[r_U, r_M], wr=[r_bk[2]])
                    if j < 4:
                        R.mm(bk[2][:, 128:256], M[:], U[:], rd=[r_U, r_M], wr=[r_bk[2]])
                    M, r_M = mr.next()
                    R.copy("act", M[:], bk[2][:, 0:128], rd=[r_bk[2]], wr=[r_M])
                    if j < 4:
                        U, r_U = ur.next()
                        R.copy("dve", U[:], bk[2][:, 128:256], rd=[r_bk[2]], wr=[r_U])
                    R.mm(bk[3][:, 0:128], M[:], W[:], rd=[r_M, r_W], wr=[r_bk[3]])
                    W2, r_W2 = wr_.next()
                    R.tt("dve", W2[:], bk[3][:, 0:128], W[:], ALU.add, rd=[r_bk[3], r_W], wr=[r_W2])
                    W, r_W = W2, r_W2
                if STOP <= 5:
                    return
                R.mm(bk[4][:, 0:128], W[:], vb[:], rd=[r_W, r_vb], wr=[r_bk[4]])
                R.mm(bk[4][:, 128:256], kbg[:], W[:], rd=[r_W, r_kbg], wr=[r_bk[4]])
                val, r_val = valr.next()
                R.copy("act", val[:], bk[4][:, 0:128], rd=[r_bk[4]], wr=[r_val])
                kc, r_kc = kcr.next()
                R.copy("dve", kc[:], bk[4][:, 128:256], rd=[r_bk[4]], wr=[r_kc])
                if STOP <= 6:
                    return
                vn, r_vn = vnr.next()
                order = (0, 1) if d == 0 else (1, 0)
                for ei, e in enumerate(order):
                    lo, hi = 64 * e, 64 * e + 64
                    R.mm(bk[3][:, 128:256], kc[:], Sb[:, h, :], rd=[r_kc, r_Sb[h]], wr=[r_bk[3]])
                    R.tt("dve", vn[lo:hi, :], val[lo:hi, :], bk[3][lo:hi, 128:256], ALU.subtract, rd=[r_val, r_bk[3]], wr=[r_vn])
                    if with_out:
                        qz, r_qz = (qa, r_qa) if e == 0 else (qb, r_qb)
                        R.mm(bk[5][:, 0:128], qz[:], Sb[:, h, :], start=(ei == 0), stop=False, rd=[r_qz, r_Sb[h]], wr=[r_bk[5]])
                        R.mm(bk[5][:, 0:128], at[lo:hi, :], vn[lo:hi, :], start=False, stop=(ei == 1), rd=[r_at, r_vn], wr=[r_bk[5]])
                    R.mm(bk[4][:, 256:384], kd[lo:hi, :], vn[lo:hi, :], rd=[r_kd, r_vn], wr=[r_bk[4]])
                    R.stt("dve", Sf[:, h, :], Sf[:, h, :], C(6 + e, t, dh), bk[4][:, 256:384], ALU.mult, ALU.add, rd=[r_S[h], r_bk[4]], wr=[r_S[h]])
                    R.copy("act", Sb[:, h, :], Sf[:, h, :], rd=[r_S[h]], wr=[r_Sb[h]])

            def run_pass(tiles, d, with_out, final):
                groups = {}
                os_, r_os = None, None
                for t in tiles:
                    gi = t // 4
                    if gi not in groups:
                        groups = {}
                        c0 = gi * 512
                        kg, r_kg = kgr.next()
                        R.dma("sp", kg[:], self.s_ck[:, c0:c0 + 512].rearrange("(h p) n -> p h n", p=128), wr=[r_kg])
                        vg, r_vg = vgr.next()
                        R.dma("sp", vg[:], self.s_cv[:, c0:c0 + 512].rearrange("(h p) n -> p h n", p=128), wr=[r_vg])
                        qg, r_qg, zg, r_zg = None, None, None, None
                        if with_out:
                            qg, r_qg = qgr.next()
                            R.dma("sp", qg[:], self.s_cq[:, c0 - TOK:c0 - TOK + 512].rearrange("(h p) n -> p h n", p=128), wr=[r_qg])
                            if final:
                                zg, r_zg = zgr.next()
                                R.dma("sp", zg[:], self.s_z[:, c0 - TOK:c0 - TOK + 512].rearrange("(h p) n -> p h n", p=128), wr=[r_zg])
                        groups[gi] = (kg, r_kg, vg, r_vg, qg, r_qg, zg, r_zg)
                    kg, r_kg, vg, r_vg, qg, r_qg, zg, r_zg = groups[gi]
                    toff = (t % 4) * 128
                    if with_out:
                        os_, r_os = osr.next()
                    for h in range(8):
                        unit(t, toff, h, d, with_out, kg, r_kg, vg, r_vg, qg, r_qg)
                        if with_out and STOP > 7:
                            R.copy(alt(), os_[:, h, :], bk[5][:, 0:128], rd=[r_bk[5]], wr=[r_os])
                    if not with_out or STOP <= 8:
                        continue
                    tl = t - 32
                    if not final:
                        R.dma("act", self.s_oX[tl * 128:(tl + 1) * 128, :], os_[:].rearrange("p a b -> p (a b)"), rd=[r_os], wr=[self.sreg("oX", tl)])
                        continue
                    ox, r_ox = oxr.next()
                    R.dma("sp", ox[:].rearrange("p a b -> p (a b)"), self.s_oX[tl * 128:(tl + 1) * 128, :], rd=[self.sreg("oX", tl)], wr=[r_ox])
                    fl = lambda a: a[:].rearrange("p a b -> p (a b)")
                    R.tt("dve", fl(os_), fl(os_), fl(ox), ALU.add, rd=[r_os, r_ox], wr=[r_os])
                    R.tt("pool", fl(sqo), fl(os_), fl(os_), ALU.mult, rd=[r_os], wr=[r_sqo])
                    ss, r_ss = ssr.next()
                    R.op("dve", lambda e, ss=ss: e.tensor_reduce(out=ss[:], in_=sqo[:], axis=AX.X, op=ALU.add), [r_sqo], [r_ss])
                    R.act(ss[:], ss[:], AF.Sqrt, bias=EPS, scale=1.0 / 128, rd=[r_ss], wr=[r_ss])
                    R.recip(ss[:], ss[:], rd=[r_ss], wr=[r_ss])
                    on, r_on = onr.next()
                    ya, r_ya = yar.next()
                    for h in range(8):
                        R.ts("dve", on[:, h, :], os_[:, h, :], ss[:, h:h + 1], None, ALU.mult, rd=[r_os, r_ss], wr=[r_on])
                        R.tr(tyb, on[:, h, :], self.identB[:], rd=[r_on], wr=[r_tyb])
                        R.stt("dve", ya[:, h, :], tyb, self.c_col(0), zg[:, h, toff:toff + 128], ALU.mult, ALU.mult,
                              rd=[r_tyb, r_zg], wr=[r_ya])
                    R.dma("act", self.s_ya[:, tl * 128:(tl + 1) * 128].rearrange("(h p) n -> p h n", p=128), ya[:], rd=[r_ya])

            NDBG = int(os.environ.get("DN_TILES", "32"))
            run_pass(list(range(0, 32))[:NDBG], 0, False, False)
            flc = self.cs[:, 1331:1332]
            R.ts("dve", Sf[:].rearrange("p a b -> p (a b)"), Sf[:].rearrange("p a b -> p (a b)"), flc, None, ALU.mult, rd=r_S, wr=r_S)
            R.ts("dve", Sb[:].rearrange("p a b -> p (a b)"), Sb[:].rearrange("p a b -> p (a b)"), flc, None, ALU.mult, rd=r_Sb, wr=r_Sb)
            run_pass(list(range(32, 64))[:NDBG], 0, True, False)
            R.memset("dve", Sf[:].rearrange("p a b -> p (a b)"), 0.0, wr=r_S)
            R.memset("pool", Sb[:].rearrange("p a b -> p (a b)"), 0.0, wr=r_Sb)
            run_pass(list(range(63, 31, -1))[32 - NDBG:], 1, True, True)
            R.emit(st)
        nc.all_engine_barrier()

def pack_consts(norm1_g, norm2_g, norm3_g, dn_norm_g, na_q_g, na_k_g, conv_w, flipped, flag=1.0):
    c = np.zeros((128, 2048), np.float32)
    p = np.arange(128)[:, None]
    f = np.arange(128)[None, :]
    same = (p // 64) == (f // 64)
    c[:, 0:128] = np.eye(128, dtype=np.float32)
    for i, m in enumerate([(f > p) & same, (f < p) & same, (f >= p) & same, (f <= p) & same]):
        c[:, 128 + 128 * i:256 + 128 * i] = np.where(m, 0.0, NEG)
    c[:, 640:768] = ((p <= f) & same)
    c[:, 768:896] = ((p >= f) & same)
    c[:, 896:1024] = (p < 64) * np.ones((1, 128))
    c[:, 1024:1152] = (p >= 64) * np.ones((1, 128))
    c[:, 1152:1280] = 1.0
    for i, gvec in enumerate([norm1_g, norm2_g, norm3_g]):
        c[:, 1280 + 16 * i:1296 + 16 * i] = gvec.reshape(16, 128).T
    c[:, 1328] = dn_norm_g
    c[:, 1329] = na_q_g
    c[:, 1330] = na_k_g
    c[:, 1331] = flag
    cw = conv_w[::-1] if flipped else conv_w
    c[:, 1344:1344 + 120] = cw.T.reshape(24, 128, 5).transpose(1, 0, 2).reshape(128, 120)
    return c


def core_layout(c, inp):
    if c < 4:
        own = inp["x_prompt"][c]
        par = np.zeros_like(own)
        pp = inp["p_prompt"][0, c]
        flipped = False
    else:
        s, half = (c - 4) // 2, (c - 4) % 2
        xs, ps = inp["x_sample"][s], inp["p_sample"][0, s]
        if half == 0:
            own, par, pp, flipped = xs[:TOK][::-1], xs[TOK:][::-1], ps[:TOK][::-1], True
        else:
            own, par, pp, flipped = xs[TOK:], xs[:TOK], ps[TOK:], False
    w_in = inp["w_in"][0]
    ba = w_in[:, 4096:4128]
    if flipped:
        ba = np.concatenate([ba[:, 8:16], ba[:, 0:8], ba[:, 24:32], ba[:, 16:24]], axis=1)
    alog = inp["dn_a_log"][0]
    dtb = inp["dn_dt_bias"][0]
    if flipped:
        alog, dtb = alog[::-1], dtb[::-1]
    coefc = np.zeros((128, 2048), np.float32)
    coefc[:, 0:1024] = np.tile(alog.reshape(1, 16), (128, 64))
    coefc[:, 1024:2048] = np.tile(dtb.reshape(1, 16), (128, 64))
    d = {
        "xo": np.ascontiguousarray(own.T), "xp": np.ascontiguousarray(par.T), "pT": np.ascontiguousarray(pp.T),
        "w_ba": np.ascontiguousarray(ba),
        "cst": pack_consts(inp["norm1_g"][0], inp["norm2_g"][0], inp["norm3_g"][0], inp["dn_norm_g"][0],
                           inp["na_q_norm_g"][0], inp["na_k_norm_g"][0], inp["dn_conv_w"][0], flipped, flag=(0.0 if c < 4 else 1.0)),
        "coefc": coefc,
    }
    return d, flipped


def na_bias_tables(rpb, flipped, seq_start):
    H = rpb.shape[0]
    out = np.full((128, H, 26, 128), NEG, np.float32)
    kc = np.arange(64)
    qc = np.arange(64)

    def fill(tile, krow0, qrow0, rows=64, pre_valid=True):
        for a in range(2):
            kr = krow0 + a
            for b in range(2):
                qr = qrow0 + b
                if not flipped:
                    lo = qr - 4
                    if seq_start:
                        lo = max(lo, 0)
                    lo = min(lo, rows - 8)
                else:
                    lo = min(qr - 3, rows - 8)
                hi = lo + 7
                if kr < lo or kr > hi or kr >= rows or (kr < 0 and seq_start):
                    continue
                dy = kr - qr
                if flipped:
                    dy = -dy
                if not flipped:
                    ws = np.clip(qc - 8, 0, 48)
                else:
                    gq = 63 - qc
                    wsg = np.clip(gq - 8, 0, 48)
                    ws = 63 - (wsg + 15)
                valid = (kc[:, None] >= ws[None, :]) & (kc[:, None] < ws[None, :] + 16)
                dx = kc[:, None] - qc[None, :]
                if flipped:
                    dx = -dx
                dxi = np.clip(dx, -15, 15) + 15
                for h in range(H):
                    blk = np.where(valid, rpb[h, dy + 7][dxi], NEG)
                    out[a * 64:(a + 1) * 64, h, tile, b * 64:(b + 1) * 64] = blk

    for t in range(5):
        fill(t, 8 - 4 + 2 * t, 8)
    for t in range(6):
        fill(5 + t, -4 + 2 * t, 0)
    for t in range(5):
        fill(11 + t, -2 + 2 * t, 2)
    for t in range(4):
        fill(16 + t, 56 + 2 * t, 60)
        fill(20 + t, 56 + 2 * t, 62)
    return out.reshape(128, H * 26 * 128)


_PROG = {}


SPLIT = True


def get_prog(debug=None, upto=99, mode=None, only=None):
    key = (debug, upto, mode, only)
    if key not in _PROG:
        p = Prog(debug, mode=mode)
        p.build(upto, only=(set(only) if only else None))
        _PROG[key] = p
    return _PROG[key]


def make_in_maps(inp):
    shared = {
        "w_in": np.ascontiguousarray(inp["w_in"][0]),
        "w_bra": np.ascontiguousarray(inp["w_branch_a"][0]), "w_brb": np.ascontiguousarray(inp["w_branch_b"][0]),
        "w_out": np.ascontiguousarray(inp["w_out"][0]), "w_fg": np.ascontiguousarray(inp["w_ff_gate"][0]),
        "w_fu": np.ascontiguousarray(inp["w_ff_up"][0]), "w_fd": np.ascontiguousarray(inp["w_ff_down"][0]),
        "w_pg": np.ascontiguousarray(inp["w_ple_gate"][0]), "w_pp": np.ascontiguousarray(inp["w_ple_proj"][0]),
    }
    maps, flips = [], []
    for c in range(8):
        d, fl = core_layout(c, inp)
        d["nab"] = na_bias_tables(inp["na_rpb"][0], fl, seq_start=(c < 4))
        d.update(shared)
        maps.append(d)
        flips.append(fl)
    return maps, flips


def kernel(**inp):
    inp = {k: np.asarray(v) for k, v in inp.items()}
    maps, flips = make_in_maps(inp)
    if SPLIT:
        pa = get_prog(mode="A", only=(1, 2, 3))
        ra = run_bass_kernel_spmd(pa.nc, maps, core_ids=list(range(8)))
        for c in range(8):
            for k in ("s_cq", "s_ck", "s_cv", "s_z", "s_yb", "s_ga", "s_gb", "s_coef"):
                maps[c][k] = np.asarray(ra.results[c][k])
        pb = get_prog(mode="B", only=(4, 5))
        res = run_bass_kernel_spmd(pb.nc, maps, core_ids=list(range(8)))
    else:
        prog = get_prog()
        res = run_bass_kernel_spmd(prog.nc, maps, core_ids=list(range(8)))
    yp = np.zeros((4, TOK, D), np.float32)
    ys = np.zeros((2, 2 * TOK, D), np.float32)
    for c in range(8):
        y = res.results[c]["yT"].T
        if flips[c]:
            y = y[::-1]
        if c < 4:
            yp[c] = y
        else:
            s, half = (c - 4) // 2, (c - 4) % 2
            ys[s, half * TOK:(half + 1) * TOK] = y
    return yp, ys
```
